# Optimizing a Trainium2 kernel written in Bass

```python
import math
import jax, jax.numpy as jnp
from jax import lax
import numpy as np

D_MODEL = 1024
BATCH = 2
SEQ = 8192
DEPTH = 2

D_MIX = D_MODEL
D_CONV = D_MIX // 4
D_ATTN = D_MIX // 2
D_SGU = D_MIX // 4
N_ATTN_HEADS = 4
ATTN_V_DIM = D_ATTN // N_ATTN_HEADS
ATTN_QK_DIM = ATTN_V_DIM // 2
Q_BLOCK = 128
CONV_WIDTH = 31
N_SGU_GROUPS = 4
SGU_GROUP_DIM = D_SGU // N_SGU_GROUPS
CHUNK = 128
D_IN = 2 * D_CONV + 3 * D_ATTN + 2 * D_SGU + D_MIX
EPS = 1e-6
NEG_INF = -1e30

kernel_name = "hybrid_conv_diffattn_sgu_block"


def _rms_norm(x, g):
    xf = x.astype(jnp.float32)
    y = xf * lax.rsqrt(jnp.mean(xf * xf, axis=-1, keepdims=True) + EPS)
    return (y * g.astype(jnp.float32)).astype(x.dtype)


def _layer_norm(x, g, b):
    xf = x.astype(jnp.float32)
    mu = jnp.mean(xf, axis=-1, keepdims=True)
    xc = xf - mu
    y = xc * lax.rsqrt(jnp.mean(xc * xc, axis=-1, keepdims=True) + EPS)
    return (y * g.astype(jnp.float32) + b.astype(jnp.float32)).astype(x.dtype)


def _conv_branch(h, conv_w, conv_b, ln_g, ln_b):
    a, gt = jnp.split(h, 2, axis=-1)
    z = a * jax.nn.sigmoid(gt)
    z = lax.conv_general_dilated(
        z, conv_w[:, None, :].astype(z.dtype),
        window_strides=(1,), padding=[(CONV_WIDTH - 1, 0)],
        dimension_numbers=("NWC", "WIO", "NWC"),
        feature_group_count=D_CONV) + conv_b
    z = _layer_norm(z, ln_g, ln_b)
    return jax.nn.silu(z)


def _diff_attention(q, k, v, lam, lam_init, subln_g):
    B, S, _ = q.shape
    q = q.reshape(B, S, N_ATTN_HEADS, 2, ATTN_QK_DIM)
    k = k.reshape(B, S, N_ATTN_HEADS, 2, ATTN_QK_DIM)
    v = v.reshape(B, S, N_ATTN_HEADS, ATTN_V_DIM)
    scale = 1.0 / math.sqrt(ATTN_QK_DIM)
    slopes = jnp.exp2(-8.0 * (jnp.arange(N_ATTN_HEADS, dtype=jnp.float32) + 1.0) / N_ATTN_HEADS)
    kpos = jnp.arange(S)
    n_blocks = S // Q_BLOCK

    def block(i):
        qb = lax.dynamic_slice_in_dim(q, i * Q_BLOCK, Q_BLOCK, axis=1)
        s = jnp.einsum('bqhcd,bkhcd->bhcqk', qb, k).astype(jnp.float32) * scale
        qpos = i * Q_BLOCK + jnp.arange(Q_BLOCK)
        rel = qpos[:, None] - kpos[None, :]
        alibi = -slopes[:, None, None] * rel.astype(jnp.float32)[None]
        s = jnp.where((rel >= 0)[None, None, None], s + alibi[None, :, None], NEG_INF)
        p = jax.nn.softmax(s, axis=-1)
        pd = p[:, :, 0] - lam * p[:, :, 1]
        return jnp.einsum('bhqk,bkhv->bqhv', pd.astype(v.dtype), v)

    o = lax.map(block, jnp.arange(n_blocks))
    o = jnp.transpose(o, (1, 0, 2, 3, 4)).reshape(B, S, N_ATTN_HEADS, ATTN_V_DIM)
    o = _rms_norm(o, subln_g) * (1.0 - lam_init)
    return o.reshape(B, S, D_ATTN)


def _sgu_branch(h, ln_g, ln_b, w_s, b_s):
    B, S, _ = h.shape
    u, vv = jnp.split(h, 2, axis=-1)
    vv = _layer_norm(vv, ln_g, ln_b)
    vv = vv.reshape(B, S // CHUNK, CHUNK, N_SGU_GROUPS, SGU_GROUP_DIM)
    w = w_s * jnp.tril(jnp.ones((CHUNK, CHUNK), w_s.dtype))[None]
    mixed = jnp.einsum('gts,bcsgd->bctgd', w, vv) + b_s.T[None, None, :, :, None]
    return u * mixed.reshape(B, S, D_SGU)


def setup_inputs(seed: int = 0) -> dict:
    key = jax.random.key(seed)
    ks = jax.random.split(key, 20)
    f32 = jnp.float32
    n = lambda k, shape: jax.random.normal(k, shape, f32)
    return {
        "x": n(ks[0], (BATCH, SEQ, D_MODEL)),
        "norm_g": 1.0 + 0.02 * n(ks[1], (DEPTH, D_MODEL)),
        "w_in": n(ks[2], (DEPTH, D_MODEL, D_IN)) * D_MODEL ** -0.5,
        "conv_w": n(ks[3], (DEPTH, CONV_WIDTH, D_CONV)) * CONV_WIDTH ** -0.5,
        "conv_b": 0.02 * n(ks[4], (DEPTH, D_CONV)),
        "conv_ln_g": 1.0 + 0.02 * n(ks[5], (DEPTH, D_CONV)),
        "conv_ln_b": 0.02 * n(ks[6], (DEPTH, D_CONV)),
        "lam_q1": 0.1 * n(ks[7], (DEPTH, ATTN_QK_DIM)),
        "lam_k1": 0.1 * n(ks[8], (DEPTH, ATTN_QK_DIM)),
        "lam_q2": 0.1 * n(ks[9], (DEPTH, ATTN_QK_DIM)),
        "lam_k2": 0.1 * n(ks[10], (DEPTH, ATTN_QK_DIM)),
        "subln_g": 1.0 + 0.02 * n(ks[11], (DEPTH, ATTN_V_DIM)),
        "sgu_ln_g": 1.0 + 0.02 * n(ks[12], (DEPTH, D_SGU)),
        "sgu_ln_b": 0.02 * n(ks[13], (DEPTH, D_SGU)),
        "w_s": n(ks[14], (DEPTH, N_SGU_GROUPS, CHUNK, CHUNK)) * CHUNK ** -0.5,
        "b_s": 1.0 + 0.02 * n(ks[15], (DEPTH, N_SGU_GROUPS, CHUNK)),
        "w_out": n(ks[16], (DEPTH, D_MIX, D_MODEL)) * D_MIX ** -0.5,
        "norm_f": 1.0 + 0.02 * n(ks[17], (D_MODEL,)),
    }


def reference(x, norm_g, w_in, conv_w, conv_b, conv_ln_g, conv_ln_b,
              lam_q1, lam_k1, lam_q2, lam_k2, subln_g,
              sgu_ln_g, sgu_ln_b, w_s, b_s, w_out, norm_f):
    splits = np.cumsum([2 * D_CONV, D_ATTN, D_ATTN, D_ATTN, 2 * D_SGU]).tolist()
    for l in range(DEPTH):
        lam_init = 0.8 - 0.6 * math.exp(-0.3 * l)
        lam = (jnp.exp(jnp.sum(lam_q1[l].astype(jnp.float32) * lam_k1[l].astype(jnp.float32)))
               - jnp.exp(jnp.sum(lam_q2[l].astype(jnp.float32) * lam_k2[l].astype(jnp.float32)))
               + lam_init)
        h = _rms_norm(x, norm_g[l])
        proj = jnp.einsum('bsd,de->bse', h, w_in[l])
        p_conv, p_q, p_k, p_v, p_sgu, gate = jnp.split(proj, splits, axis=-1)
        y_a = _conv_branch(p_conv, conv_w[l], conv_b[l], conv_ln_g[l], conv_ln_b[l])
        y_b = _diff_attention(p_q, p_k, p_v, lam, lam_init, subln_g[l])
        y_c = _sgu_branch(p_sgu, sgu_ln_g[l], sgu_ln_b[l], w_s[l], b_s[l])
        y = jnp.concatenate([y_a, y_b, y_c], axis=-1) * jax.nn.silu(gate)
        x = x + jnp.einsum('bse,ed->bsd', y, w_out[l])
    return _rms_norm(x, norm_f)
```

```python
import math
import numpy as np
import ml_dtypes
import concourse.bass as bass
import concourse.mybir as mybir
from concourse.bass_utils import run_bass_kernel_spmd

F32 = mybir.dt.float32
BF16 = mybir.dt.bfloat16
ALU = mybir.AluOpType
AF = mybir.ActivationFunctionType
AX = mybir.AxisListType

D_MODEL = 1024
D_IN = 3584
N_CORES = 8
EPS = 1e-6
NEG = -30000.0
ENGS = ("tensor", "vector", "scalar", "gpsimd", "sync")


class Res:
    __slots__ = ("w", "r", "name", "multi", "wl")

    def __init__(self, name="", multi=False):
        self.w = None
        self.r = []
        self.name = name
        self.multi = multi
        self.wl = []


class Prog:
    def __init__(self, nc):
        self.nc = nc
        self.q = {e: [] for e in ENGS}
        self.sems = {}
        self.cnt = {}
        self.seen = {e: {} for e in ENGS}
        self.pid = None
        self.need_pid = set()
        for e in ENGS:
            self._mk("e_" + e)

    def _mk(self, key):
        self.sems[key] = self.nc.alloc_semaphore(name=key)
        self.cnt[key] = 0

    def dma_sem(self, key):
        if key not in self.sems:
            self._mk(key)
        return key

    def _deps(self, eng, reads, writes, acc):
        need = {}

        def add(d):
            if d is None:
                return
            k, v = d
            if need.get(k, 0) < v:
                need[k] = v
        for r in reads:
            add(r.w)
            for d in r.wl:
                add(d)
        for w in writes:
            if w.multi:
                pass
            elif not (acc and eng == "tensor" and w.w is not None and w.w[0] == "e_tensor"):
                add(w.w)
            for d in w.r:
                add(d)
        out = []
        for k, v in need.items():
            if self.seen[eng].get(k, 0) < v:
                self.seen[eng][k] = v
                out.append((k, v))
        return out

    def _finish(self, key, inc, reads, writes):
        self.cnt[key] += inc
        d = (key, self.cnt[key])
        for r in reads:
            r.r.append(d)
        for w in writes:
            if w.multi:
                w.wl.append(d)
            else:
                w.w = d
                w.r = []
        return d

    def op(self, eng, fn, reads=(), writes=(), acc=False):
        waits = self._deps(eng, reads, writes, acc)
        key = "e_" + eng
        sems = self.sems

        def thunk(e, waits=waits, fn=fn, key=key):
            for k, v in waits:
                e.wait_ge(sems[k], v)
            fn(e).then_inc(sems[key], 1)
        self.q[eng].append(thunk)
        self._finish(key, 1, reads, writes)

    def dma(self, eng, semkey, out, in_, reads=(), writes=()):
        self.dma_sem(semkey)
        waits = self._deps(eng, reads, writes, False)
        sems = self.sems

        def thunk(e, waits=waits):
            for k, v in waits:
                e.wait_ge(sems[k], v)
            o = out(e, self.pid) if callable(out) else out
            i = in_(e, self.pid) if callable(in_) else in_
            e.dma_start(out=o, in_=i).then_inc(sems[semkey], 16)
        self.q[eng].append(thunk)
        self._finish(semkey, 16, reads, writes)

    def dma_dyn(self, semkey, fn, reads=(), writes=(), eng="sync"):
        self.dma_sem(semkey)
        self.need_pid.add(eng)
        waits = self._deps(eng, reads, writes, False)
        sems = self.sems

        def thunk(e, waits=waits):
            for k, v in waits:
                e.wait_ge(sems[k], v)
            for cid in range(N_CORES):
                o, i = fn(cid)
                with e.If(self.pid == cid):
                    e.dma_start(out=o, in_=i).then_inc(sems[semkey], 16)
        self.q[eng].append(thunk)
        self._finish(semkey, 16, reads, writes)

    def cc(self, semkey, kind, ins, outs, groups, reads=(), writes=()):
        self.dma_sem(semkey)
        waits = self._deps("gpsimd", reads, writes, False)
        sems = self.sems

        def thunk(e, waits=waits):
            for k, v in waits:
                e.wait_ge(sems[k], v)
            e.collective_compute(kind, mybir.AluOpType.bypass, replica_groups=groups,
                                 ins=ins, outs=outs).then_inc(sems[semkey])
        self.q["gpsimd"].append(thunk)
        self._finish(semkey, 1, reads, writes)

    def drain(self):
        need = {k: v for k, v in self.cnt.items() if k.startswith("d_") and v > 0}
        sems = self.sems

        def thunk(e, need=need):
            for k, v in need.items():
                e.wait_ge(sems[k], v)
        self.q["sync"].append(thunk)
        eneed = {k: v for k, v in self.cnt.items() if (k.startswith("e_") or k.startswith("d_")) and v > 0}

        def bthunk(e, eneed=eneed):
            for k, v in eneed.items():
                e.wait_ge(sems[k], v)
        for ename in ENGS:
            self.q[ename].append(bthunk)
            for k, v in eneed.items():
                if self.seen[ename].get(k, 0) < v:
                    self.seen[ename][k] = v

    def final_wait(self, eng, resources):
        need = {}
        for r in resources:
            for d in ([r.w] if r.w else []) + r.r:
                if need.get(d[0], 0) < d[1]:
                    need[d[0]] = d[1]
        sems = self.sems

        def thunk(e):
            for k, v in need.items():
                e.wait_ge(sems[k], v)
        self.q[eng].append(thunk)

    def emit(self):
        with self.nc.Block(no_gpsimd_drain=True) as block:
            for ename in ENGS:
                lst = self.q[ename]
                if not lst:
                    continue

                def body(e, lst=lst, ename=ename):
                    if ename in self.need_pid:
                        self.pid = e.partition_id()
                    for t in lst:
                        t(e)
                getattr(block, ename)(body)
        self.q = {e: [] for e in ENGS}
        self.need_pid = set()


from contextlib import ExitStack


class Ctx:
    N = [0]

    def __init__(self, nc):
        self.nc = nc
        self.es = ExitStack()

    def sb(self, shape, dt, name=None):
        Ctx.N[0] += 1
        return self.es.enter_context(self.nc.sbuf_tensor(name or f"sb{Ctx.N[0]}", list(shape), dt))

    def ps(self, shape, dt=F32, name=None):
        Ctx.N[0] += 1
        return self.es.enter_context(self.nc.psum_tensor(name or f"ps{Ctx.N[0]}", list(shape), dt))


def emit_rsqrt(p, out, in_, scale, reads, writes):
    p.op("vector", lambda e: e.tensor_scalar(out=out, in0=in_, scalar1=scale, scalar2=EPS,
                                             op0=ALU.mult, op1=ALU.add), reads=reads, writes=writes)
    p.op("scalar", lambda e: e.activation(out=out, in_=out, func=AF.Ln), reads=writes, writes=writes)
    p.op("scalar", lambda e: e.activation(out=out, in_=out, func=AF.Exp, scale=-0.5), reads=writes, writes=writes)


KC = D_MODEL // 128
GROUPS = [list(range(N_CORES))]


def build_fused(T, S, depth):
    nc = bass.Bass("TRN2", target_bir_lowering=False)
    NB = T // 128
    NTT = T // 512
    NQT = S // 512
    NKB = S // 128
    NR = 4 * NQT
    CPB = S // T
    assert CPB == 4
    H = 32
    RPB = 16384 // T
    ext = lambda n, sh, dt=F32: nc.dram_tensor(n, sh, dt, kind="ExternalInput").ap()
    x_d = ext("x", [T, D_MODEL])
    w_in_d = ext("w_in", [depth, D_MODEL, D_IN])
    g_d = ext("g", [depth, 128, KC])
    lamp_d = ext("lamp", [depth, 128, 256])
    gsub_d = ext("gsub", [depth, 128, 128])
    abias_d = ext("abias", [128, NR])
    cst_d = ext("cst", [depth, 128, 2])
    tri_d = ext("tri", [128, 128], BF16)
    idn_d = ext("idn", [128, 128], BF16)
    idf_d = ext("idf", [128, 128])
    wo_d = ext("wo", [depth, D_MODEL, D_MODEL])
    cw_d = ext("cw", [depth, 128, 2, 31])
    cpar_d = ext("cpar", [depth, 128, 6])
    sln_d = ext("sln", [depth, 128, 512])
    wsT_d = ext("wsT", [depth, 128, 4, 128])
    trilm_d = ext("trilm", [128, 128])
    bsb_d = ext("bsb", [depth, 128, 2, 128])
    hmask_d = ext("hmask", [128, 1])
    nf_d = ext("nf", [128, D_MODEL])
    onesm_d = ext("onesm", [128, 128])
    out_d = nc.dram_tensor("out", [T, D_MODEL], F32, kind="ExternalOutput").ap()
    pT_s = [nc.dram_tensor(f"pT_s{l}", [14 * 128, T], F32) for l in range(depth)]
    vv_s = [nc.dram_tensor(f"vv_s{l}", [T, 256], F32) for l in range(depth)]
    agq_in = [nc.dram_tensor(f"agq_in{l}", [1536, T], BF16) for l in range(depth)]
    agq_out = [nc.dram_tensor(f"agq_out{l}", [N_CORES * 1536, T], BF16) for l in range(depth)]
    agh_in = [nc.dram_tensor(f"agh_in{l}", [512, H], F32) for l in range(depth)]
    agh_out = [nc.dram_tensor(f"agh_out{l}", [N_CORES * 512, H], F32) for l in range(depth)]
    agy_in = [[nc.dram_tensor(f"agy_in{l}_{r}", [128, T], F32) for r in range(4)] for l in range(depth)]
    agy_out = [[nc.dram_tensor(f"agy_out{l}_{r}", [N_CORES * 128, T], F32) for r in range(4)] for l in range(depth)]
    x1_s = nc.dram_tensor("x1_s", [T, D_MODEL], F32)
    R = lambda: Res(multi=True)
    r_pT = [R() for _ in range(depth)]
    r_vv = [R() for _ in range(depth)]
    r_agq_in = [R() for _ in range(depth)]
    r_agq_out = [Res() for _ in range(depth)]
    r_agh_in = [R() for _ in range(depth)]
    r_agh_out = [Res() for _ in range(depth)]
    r_agy_in = [[R() for _ in range(4)] for _ in range(depth)]
    r_agy_out = [[Res() for _ in range(4)] for _ in range(depth)]
    r_x1 = R()
    r_out = R()

    p = Prog(nc)

    def load(c, eng, name, shape, src, dt=F32):
        t = c.sb(shape, dt)
        r = Res()
        p.dma(eng, "d_" + name, t[:], src, writes=[r])
        return t, r

    def phase_P(l):
        xsrc, r_xsrc = (x_d, None) if l == 0 else (x1_s.ap(), r_x1)
        c = Ctx(nc)
        with c.es:
            gt, r_g = load(c, "sync", "g", [128, KC], g_d[l])
            idn, r_idn = load(c, "sync", "idn", [128, 128], idn_d, BF16)
            xt = [c.sb([128, D_MODEL], F32) for _ in range(2)]; r_xt = [Res(), Res()]
            jk = [c.sb([128, D_MODEL], F32) for _ in range(2)]; r_jk = [Res(), Res()]
            ss = [c.sb([128, 1], F32) for _ in range(2)]; r_ss = [Res(), Res()]
            hb = [c.sb([128, D_MODEL], BF16) for _ in range(2)]; r_hb = [Res(), Res()]
            pst = [c.ps([128, 1024], BF16) for _ in range(2)]; r_pst = [Res(), Res()]
            hT = c.sb([128, KC, T], BF16); r_hT = Res()
            for b in range(NB):
                s = b % 2
                p.dma("sync", f"d_xt{s}", xt[s][:], xsrc[b * 128:(b + 1) * 128, :],
                      reads=[r_xsrc] if r_xsrc else [], writes=[r_xt[s]])
                p.op("vector", lambda e, s=s: e.memset(ss[s][:], 0.0), writes=[r_ss[s]])
                p.op("scalar", lambda e, s=s: e.activation(out=jk[s][:], in_=xt[s][:], func=AF.Square, accum_out=ss[s][:]),
                     reads=[r_xt[s]], writes=[r_jk[s], r_ss[s]])
                emit_rsqrt(p, ss[s][:], ss[s][:], 1.0 / D_MODEL, [r_ss[s]], [r_ss[s]])
                p.op("vector", lambda e, s=s: e.tensor_scalar(out=hb[s][:], in0=xt[s][:], scalar1=ss[s][:, 0:1], scalar2=None,
                                                              op0=ALU.mult), reads=[r_xt[s], r_ss[s]], writes=[r_hb[s]])
                for k in range(KC):
                    p.op("tensor", lambda e, s=s, k=k: e.transpose(out=pst[s][:, k * 128:(k + 1) * 128],
                                                                   in_=hb[s][:, k * 128:(k + 1) * 128], identity=idn[:]),
                         reads=[r_hb[s], r_idn], writes=[r_pst[s]], acc=True)
                if b % 2 == 0:
                    p.op("vector", lambda e, s=s, b=b: e.tensor_copy(out=hT[:, :, b * 128:(b + 1) * 128],
                                                                     in_=pst[s][:].rearrange("p (k t) -> p k t", k=KC)),
                         reads=[r_pst[s]], writes=[r_hT])
                else:
                    p.op("scalar", lambda e, s=s, b=b: e.copy(out=hT[:, :, b * 128:(b + 1) * 128],
                                                              in_=pst[s][:].rearrange("p (k t) -> p k t", k=KC)),
                         reads=[r_pst[s]], writes=[r_hT])
            wf = [c.sb([128, KC, 512], F32) for _ in range(2)]; r_wf = [Res(), Res()]
            wb = [c.sb([128, KC, 512], BF16) for _ in range(2)]; r_wb = [Res(), Res()]
            ps = [c.ps([128, 512]) for _ in range(4)]; r_ps = [Res() for _ in range(4)]
            ob = [c.sb([128, 512], F32) for _ in range(4)]; r_ob = [Res() for _ in range(4)]
            obh = [c.sb([128, 512], BF16) for _ in range(4)]; r_obh = [Res() for _ in range(4)]
            wv = w_in_d[l].rearrange("(k p) n -> p k n", p=128)
            agq = agq_in[l].ap()
            vdst = agq[1024:1536, :].rearrange("(h r) (f c) -> h r f c", h=4, c=128)
            it = [0]

            def evac(i, dst_f32, dst_bf, ncols, rdst):
                if dst_bf is None:
                    if it[0] % 2 == 0:
                        p.op("vector", lambda e: e.tensor_copy(out=ob[i][:, 0:ncols], in_=ps[i][:, 0:ncols]),
                             reads=[r_ps[i]], writes=[r_ob[i]])
                    else:
                        p.op("scalar", lambda e: e.copy(out=ob[i][:, 0:ncols], in_=ps[i][:, 0:ncols]),
                             reads=[r_ps[i]], writes=[r_ob[i]])
                    p.dma("sync", f"d_o{i}", dst_f32, ob[i][:, 0:ncols], reads=[r_ob[i]], writes=[rdst])
                else:
                    if it[0] % 2 == 0:
                        p.op("vector", lambda e: e.tensor_copy(out=obh[i][:, 0:ncols], in_=ps[i][:, 0:ncols]),
                             reads=[r_ps[i]], writes=[r_obh[i]])
                    else:
                        p.op("scalar", lambda e: e.copy(out=obh[i][:, 0:ncols], in_=ps[i][:, 0:ncols]),
                             reads=[r_ps[i]], writes=[r_obh[i]])
                    p.dma("gpsimd", f"d_oh{i}", dst_bf, obh[i][:, 0:ncols], reads=[r_obh[i]], writes=[rdst])

            def form_b(s, m, dst, bf, rdst):
                for tt in range(NTT):
                    i = it[0] % 4
                    it[0] += 1
                    for k in range(KC):
                        p.op("tensor", lambda e, i=i, k=k, tt=tt: e.matmul(
                            ps[i][:], lhsT=wb[s][:, k, m * 128:(m + 1) * 128], rhs=hT[:, k, tt * 512:(tt + 1) * 512],
                            start=(k == 0), stop=(k == KC - 1)), reads=[r_hT, r_wb[s]], writes=[r_ps[i]], acc=True)
                    d = dst[:, tt * 512:(tt + 1) * 512]
                    evac(i, None if bf else d, d if bf else None, 512, rdst)

            def form_a(s, c0, ncols, dstfn, bf, rdst):
                for b in range(NB):
                    i = it[0] % 4
                    it[0] += 1
                    for k in range(KC):
                        p.op("tensor", lambda e, i=i, k=k, b=b: e.matmul(
                            ps[i][:, 0:ncols], lhsT=hT[:, k, b * 128:(b + 1) * 128], rhs=wb[s][:, k, c0:c0 + ncols],
                            start=(k == 0), stop=(k == KC - 1)), reads=[r_hT, r_wb[s]], writes=[r_ps[i]], acc=True)
                    d = dstfn(b)
                    evac(i, None if bf else d, d if bf else None, ncols, rdst)

            pT = pT_s[l].ap()
            for gi, n in enumerate([0, 1, 2, 3, 4, 5, 6]):
                s = gi % 2
                p.dma("sync", f"d_w{s}", wf[s][:], wv[:, :, n * 512:(n + 1) * 512], writes=[r_wf[s]])
                for k in range(KC):
                    if k % 2 == 0:
                        p.op("vector", lambda e, s=s, k=k: e.tensor_scalar(out=wb[s][:, k, :], in0=wf[s][:, k, :],
                                                                           scalar1=gt[:, k:k + 1], scalar2=None, op0=ALU.mult),
                             reads=[r_wf[s], r_g], writes=[r_wb[s]])
                    else:
                        p.op("scalar", lambda e, s=s, k=k: e.activation(out=wb[s][:, k, :], in_=wf[s][:, k, :],
                                                                        func=AF.Identity, scale=gt[:, k:k + 1]),
                             reads=[r_wf[s], r_g], writes=[r_wb[s]])
                if n == 0:
                    for m in range(4):
                        form_b(s, m, pT[m * 128:(m + 1) * 128, :], False, r_pT[l])
                    p.dma("sync", "d_halo", agh_in[l].ap(), pT[0:512, T - H:T], reads=[r_pT[l]], writes=[r_agh_in[l]])
                    p.cc("c_h", "AllGather", [agh_in[l].ap().opt()], [agh_out[l].ap().opt()], GROUPS,
                         reads=[r_agh_in[l]], writes=[r_agh_out[l]])
                elif n in (1, 2):
                    for m in range(4):
                        form_b(s, m, agq[(n - 1) * 512 + m * 128:(n - 1) * 512 + (m + 1) * 128, :], True, r_agq_in[l])
                elif n == 3:
                    form_a(s, 0, 512, lambda b: vdst[:, b * RPB:(b + 1) * RPB, :, :].rearrange("h r f c -> (r f) h c"), True,
                           r_agq_in[l])
                elif n == 4:
                    for m in range(2):
                        form_b(s, m, pT[(4 + m) * 128:(5 + m) * 128, :], False, r_pT[l])
                    form_a(s, 256, 256, lambda b: vv_s[l].ap()[b * 128:(b + 1) * 128, :], False, r_vv[l])
                else:
                    for m in range(4):
                        ch = 6 + (n - 5) * 4 + m
                        form_b(s, m, pT[ch * 128:(ch + 1) * 128, :], False, r_pT[l])
            p.drain()
            p.emit()

    def phase_A(l):
        c = Ctx(nc)
        with c.es:
            qT = c.sb([128, S], BF16); r_q = Res()
            kT = c.sb([128, S], BF16); r_k = Res()
            Vp = c.sb([128, NKB, 129], BF16); r_v = Res()
            gq = agq_out[l].ap()
            qk_view = gq.rearrange("(o p) t -> o p t", p=128)
            v_view = gq.rearrange("(o blk r8) (f c) -> o (r8 f) blk c", r8=RPB, blk=NB, c=128)
            p.op("vector", lambda e: e.memset(Vp[:, :, 128:129], 1.0), writes=[r_v])
            qT3 = qT[:].rearrange("p (r t) -> p r t", r=CPB)
            kT3 = kT[:].rearrange("p (r t) -> p r t", r=CPB)
            for rr in range(CPB):
                def blk(cid, sect, rr=rr):
                    return ((cid // 4) * 4 + rr) * 12 + sect * 4 + cid % 4
                p.dma_dyn("d_q", lambda cid, rr=rr, blk=blk: (qT[:, rr * T:(rr + 1) * T], qk_view[blk(cid, 0), :, :]),
                          reads=[r_agq_out[l]], writes=[r_q])
                p.dma_dyn("d_k", lambda cid, rr=rr, blk=blk: (kT[:, rr * T:(rr + 1) * T], qk_view[blk(cid, 1), :, :]),
                          reads=[r_agq_out[l]], writes=[r_k])
                p.dma_dyn("d_v", lambda cid, rr=rr, blk=blk: (Vp[:, rr * NB:(rr + 1) * NB, 0:128], v_view[blk(cid, 2), :, :, :]),
                          reads=[r_agq_out[l]], writes=[r_v])
            lamp, r_lamp = load(c, "sync", "lamp", [128, 256], lamp_d[l])
            gsub, r_gsub = load(c, "sync", "gsub", [128, 128], gsub_d[l])
            abias, r_ab = load(c, "sync", "ab", [128, NR], abias_d)
            cst, r_cst = load(c, "sync", "cst", [128, 2], cst_d[l])
            tri, r_tri = load(c, "sync", "tri", [128, 128], tri_d, BF16)
            idn, r_idn = load(c, "sync", "idn", [128, 128], idn_d, BF16)
            idf, r_idf = load(c, "sync", "idf", [128, 128], idf_d)
            ltmp = c.sb([128, 2, 64], F32); r_ltmp = Res()
            lsum = c.sb([128, 2], F32); r_lsum = Res()
            lexp = c.sb([128, 2], F32); r_lexp = Res()
            nlam = c.sb([128, 1], F32); r_nlam = Res()
            lv = lamp[:].rearrange("p (a b d) -> p a b d", a=2, b=2)
            p.op("vector", lambda e: e.tensor_tensor(out=ltmp[:], in0=lv[:, :, 0, :], in1=lv[:, :, 1, :], op=ALU.mult),
                 reads=[r_lamp], writes=[r_ltmp])
            p.op("vector", lambda e: e.reduce_sum(out=lsum[:], in_=ltmp[:], axis=AX.X), reads=[r_ltmp], writes=[r_lsum])
            p.op("scalar", lambda e: e.activation(out=lexp[:], in_=lsum[:], func=AF.Exp), reads=[r_lsum], writes=[r_lexp])
            p.op("vector", lambda e: e.tensor_tensor(out=nlam[:], in0=lexp[:, 1:2], in1=lexp[:, 0:1], op=ALU.subtract),
                 reads=[r_lexp], writes=[r_nlam])
            p.op("vector", lambda e: e.tensor_scalar(out=nlam[:], in0=nlam[:], scalar1=cst[:, 0:1], scalar2=None,
                                                     op0=ALU.subtract), reads=[r_nlam, r_cst], writes=[r_nlam])
            gs2 = c.sb([128, 128], F32); r_gs2 = Res()
            p.op("vector", lambda e: e.tensor_scalar(out=gs2[:], in0=gsub[:], scalar1=cst[:, 1:2], scalar2=None,
                                                     op0=ALU.mult), reads=[r_gsub, r_cst], writes=[r_gs2])
            psS = [c.ps([128, 1024]) for _ in range(2)]; r_psS = [Res(), Res()]
            psO = [c.ps([128, 512]) for _ in range(3)]; r_psO = [Res() for _ in range(3)]
            psT = c.ps([128, 512]); r_psT = Res()
            NPT = 3
            pT = [c.sb([128, 1024], BF16) for _ in range(NPT)]; r_pT_ = [Res() for _ in range(NPT)]

            def oacc(a, lo=0, hi=129):
                return psO[a // 3][:, (a % 3) * 170 + lo:(a % 3) * 170 + hi], r_psO[a // 3]

            Osb = [c.sb([128, 8, 129], F32) for _ in range(2)]; r_Osb = [Res(), Res()]
            rl = [c.sb([128, 8], F32) for _ in range(2)]; r_rl = [Res(), Res()]
            c2 = [c.sb([128, 4], F32) for _ in range(2)]; r_c2 = [Res(), Res()]
            t1 = [c.sb([128, 4, 128], F32) for _ in range(2)]; r_t1 = [Res(), Res()]
            ob = [c.sb([128, 4, 128], F32) for _ in range(2)]; r_o = [Res(), Res()]
            jk = [c.sb([128, 4, 128], F32) for _ in range(2)]; r_jk = [Res(), Res()]
            ssq = [c.sb([128, 4], F32) for _ in range(2)]; r_ssq = [Res(), Res()]
            yo = [c.sb([128, 4, 128], F32) for _ in range(2)]; r_yo = [Res(), Res()]
            ybs = [c.sb([128, 512], F32) for _ in range(2)]; r_ybs = [Res(), Res()]
            steps = [(qt, kb) for qt in range(NQT) for kb in range(4 * qt + 4)]

            def rec_QK(idx):
                qt, kb = steps[idx]
                s = idx % 2
                q0 = qt * 512
                r = kb - 4 * qt
                k0 = kb * 128
                off = max(r, 0) * 128
                for cm in range(2):
                    pr = slice(cm * 64, cm * 64 + 64)
                    if r < 0:
                        p.op("tensor", lambda e, s=s, cm=cm, pr=pr, k0=k0, q0=q0: e.matmul(
                            psS[s][:, cm * 512:(cm + 1) * 512], lhsT=kT[pr, k0:k0 + 128], rhs=qT[pr, q0:q0 + 512],
                            start=True, stop=True), reads=[r_k, r_q], writes=[r_psS[s]], acc=True)
                    else:
                        p.op("tensor", lambda e, s=s, cm=cm, off=off: e.matmul(
                            psS[s][:, cm * 512 + off:cm * 512 + off + 128], lhsT=idn[:], rhs=tri[:],
                            start=True, stop=False), reads=[r_idn, r_tri], writes=[r_psS[s]], acc=True)
                        p.op("tensor", lambda e, s=s, cm=cm, pr=pr, k0=k0, q0=q0, off=off: e.matmul(
                            psS[s][:, cm * 512 + off:cm * 512 + off + 128], lhsT=kT[pr, k0:k0 + 128],
                            rhs=qT[pr, q0 + off:q0 + off + 128], start=False, stop=True),
                            reads=[r_k, r_q], writes=[r_psS[s]], acc=True)
                        if r < 3:
                            p.op("tensor", lambda e, s=s, cm=cm, pr=pr, k0=k0, q0=q0, off=off: e.matmul(
                                psS[s][:, cm * 512 + off + 128:(cm + 1) * 512], lhsT=kT[pr, k0:k0 + 128],
                                rhs=qT[pr, q0 + off + 128:q0 + 512], start=True, stop=True),
                                reads=[r_k, r_q], writes=[r_psS[s]], acc=True)

            def rec_EXP(idx):
                qt, kb = steps[idx]
                s = idx % 2
                i = idx % NPT
                r = kb - 4 * qt
                off = max(r, 0) * 128
                d = r + NR - 4
                sv = psS[s][:].rearrange("p (c n) -> p c n", c=2)[:, :, off:512]
                pv = pT[i][:].rearrange("p (c n) -> p c n", c=2)[:, :, off:512]
                p.op("scalar", lambda e, sv=sv, pv=pv, d=d: e.activation(out=pv, in_=sv, func=AF.Exp,
                                                                        bias=abias[:, d:d + 1], scale=0.125),
                     reads=[r_psS[s], r_ab], writes=[r_pT_[i]])

            def rec_PV(idx):
                qt, kb = steps[idx]
                i = idx % NPT
                r = kb - 4 * qt
                for qb in range(max(r, 0), 4):
                    for cm in range(2):
                        oa, r_oa = oacc(cm * 4 + qb)
                        p.op("tensor", lambda e, oa=oa, i=i, cm=cm, qb=qb, kb=kb: e.matmul(
                            oa, lhsT=pT[i][:, cm * 512 + qb * 128:cm * 512 + qb * 128 + 128], rhs=Vp[:, kb, :],
                            start=False, stop=False, skip_group_check=True),
                            reads=[r_pT_[i], r_v], writes=[r_oa], acc=True)

            def zero_psO():
                for a in range(3):
                    p.op("vector", lambda e, a=a: e.memset(psO[a][:], 0.0), writes=[r_psO[a]])

            def stage_A(qt):
                w = qt % 2
                for a in range(8):
                    oa, r_oa = oacc(a)
                    if a % 2 == 0:
                        p.op("vector", lambda e, w=w, a=a, oa=oa: e.tensor_copy(out=Osb[w][:, a, :], in_=oa),
                             reads=[r_oa], writes=[r_Osb[w]])
                    else:
                        p.op("scalar", lambda e, w=w, a=a, oa=oa: e.copy(out=Osb[w][:, a, :], in_=oa),
                             reads=[r_oa], writes=[r_Osb[w]])
                zero_psO()

            def stage_B(qt):
                w = qt % 2
                p.op("vector", lambda e: e.reciprocal(out=rl[w][:], in_=Osb[w][:, :, 128]), reads=[r_Osb[w]], writes=[r_rl[w]])
                p.op("vector", lambda e: e.tensor_scalar(out=c2[w][:], in0=rl[w][:, 4:8], scalar1=nlam[:, 0:1], scalar2=None,
                                                         op0=ALU.mult), reads=[r_rl[w], r_nlam], writes=[r_c2[w]])
                for qb in range(4):
                    p.op("vector", lambda e, qb=qb: e.tensor_scalar(out=t1[w][:, qb, :], in0=Osb[w][:, qb, 0:128],
                                                                    scalar1=rl[w][:, qb:qb + 1], scalar2=None, op0=ALU.mult),
                         reads=[r_Osb[w], r_rl[w]], writes=[r_t1[w]])
                for qb in range(4):
                    p.op("vector", lambda e, qb=qb: e.scalar_tensor_tensor(
                        out=ob[w][:, qb, :], in0=Osb[w][:, 4 + qb, 0:128], scalar=c2[w][:, qb:qb + 1], in1=t1[w][:, qb, :],
                        op0=ALU.mult, op1=ALU.add), reads=[r_Osb[w], r_c2[w], r_t1[w]], writes=[r_o[w]])
                p.op("vector", lambda e: e.tensor_tensor(out=jk[w][:], in0=ob[w][:], in1=ob[w][:], op=ALU.mult),
                     reads=[r_o[w]], writes=[r_jk[w]])
                p.op("vector", lambda e: e.reduce_sum(out=ssq[w][:], in_=jk[w][:], axis=AX.X), reads=[r_jk[w]], writes=[r_ssq[w]])
                p.op("vector", lambda e: e.tensor_scalar(out=ssq[w][:], in0=ssq[w][:], scalar1=1.0 / 128, scalar2=EPS,
                                                         op0=ALU.mult, op1=ALU.add), reads=[r_ssq[w]], writes=[r_ssq[w]])

            def stage_C(qt):
                w = qt % 2
                p.op("scalar", lambda e: e.activation(out=ssq[w][:], in_=ssq[w][:], func=AF.Ln), reads=[r_ssq[w]], writes=[r_ssq[w]])
                p.op("scalar", lambda e: e.activation(out=ssq[w][:], in_=ssq[w][:], func=AF.Exp, scale=-0.5),
                     reads=[r_ssq[w]], writes=[r_ssq[w]])

            def stage_D(qt):
                w = qt % 2
                for qb in range(4):
                    p.op("vector", lambda e, qb=qb: e.scalar_tensor_tensor(
                        out=yo[w][:, qb, :], in0=ob[w][:, qb, :], scalar=ssq[w][:, qb:qb + 1], in1=gs2[:],
                        op0=ALU.mult, op1=ALU.mult), reads=[r_o[w], r_ssq[w], r_gs2], writes=[r_yo[w]])

            def stage_E(qt):
                w = qt % 2
                q0 = qt * 512
                for qb in range(4):
                    p.op("tensor", lambda e, qb=qb: e.transpose(out=psT[:, qb * 128:(qb + 1) * 128], in_=yo[w][:, qb, :],
                                                                identity=idf[:]),
                         reads=[r_yo[w], r_idf], writes=[r_psT], acc=True)
                p.op("vector", lambda e: e.tensor_copy(out=ybs[w][:], in_=psT[:]), reads=[r_psT], writes=[r_ybs[w]])
                rq = q0 // T
                p.dma("sync", f"d_yb{w}", agy_in[l][rq].ap()[:, (q0 % T):(q0 % T) + 512], ybs[w][:], reads=[r_ybs[w]],
                      writes=[r_agy_in[l][rq]])
                if (q0 + 512) % T == 0:
                    p.cc(f"c_y{rq}", "AllGather", [agy_in[l][rq].ap().opt()], [agy_out[l][rq].ap().opt()], GROUPS,
                         reads=[r_agy_in[l][rq]], writes=[r_agy_out[l][rq]])

            pending = []
            zero_psO()
            rec_QK(0)
            for idx in range(len(steps)):
                if idx + 1 < len(steps):
                    rec_QK(idx + 1)
                rec_EXP(idx)
                rec_PV(idx)
                if pending:
                    pending.pop(0)()
                qt, kb = steps[idx]
                if kb == 4 * qt + 3:
                    stage_A(qt)
                    pending += [lambda qt=qt: stage_B(qt), lambda: None, lambda qt=qt: stage_C(qt),
                                lambda qt=qt: stage_D(qt), lambda: None, lambda qt=qt: stage_E(qt)]
            while pending:
                pending.pop(0)()
            p.drain()
            p.emit()

    def phase_C1(l):
        c = Ctx(nc)
        with c.es:
            p.cc("c_q", "AllGather", [agq_in[l].ap().opt()], [agq_out[l].ap().opt()], GROUPS,
                 reads=[r_agq_in[l]], writes=[r_agq_out[l]])
            cw, r_cw = load(c, "sync", "cw", [128, 2, 31], cw_d[l])
            cpar, r_cpar = load(c, "sync", "cpar", [128, 6], cpar_d[l])
            sln, r_sln = load(c, "sync", "sln", [128, 512], sln_d[l])
            wsT, r_wsT = load(c, "sync", "wsT", [128, 4, 128], wsT_d[l])
            trilm, r_trilm = load(c, "sync", "trilm", [128, 128], trilm_d)
            bsb, r_bsb = load(c, "sync", "bsb", [128, 2, 128], bsb_d[l])
            onesm, r_onesm = load(c, "sync", "onesm", [128, 128], onesm_d)
            hmask, r_hmask = load(c, "sync", "hmask", [128, 1], hmask_d)
            wsb = c.sb([128, 4, 128], BF16); r_wsb = Res()
            for g in range(4):
                p.op("vector", lambda e, g=g: e.tensor_tensor(out=wsb[:, g, :], in0=wsT[:, g, :], in1=trilm[:], op=ALU.mult),
                     reads=[r_wsT, r_trilm], writes=[r_wsb])
            az = [c.sb([128, 2, 512 + H], F32) for _ in range(2)]; r_az = [Res(), Res()]
            gz = [c.sb([128, 2, 512 + H], F32) for _ in range(2)]; r_gz = [Res(), Res()]
            g4 = [c.sb([128, 4, 512], F32) for _ in range(2)]; r_g4 = [Res(), Res()]
            uT = [c.sb([128, 2, 512], F32) for _ in range(2)]; r_uT = [Res(), Res()]
            vvt = [c.sb([128, 4, 256], F32) for _ in range(2)]; r_vvt = [Res(), Res()]
            acc = c.sb([128, 2, 512], F32); r_acc = [Res(), Res()]
            sq = c.sb([128, 2, 512], F32); r_sq = Res()
            mean = c.sb([128, 512], F32); r_mean = Res()
            m2 = c.sb([128, 512], F32); r_m2 = Res()
            var = c.sb([128, 512], F32); r_var = Res()
            cn = c.sb([128, 2, 512], F32); r_cn = [Res(), Res()]
            jk = c.sb([128, 256], F32); r_jk = Res()
            ssv = c.sb([128, 4], F32); r_ssv = Res()
            s1 = c.sb([128, 4], F32); r_s1 = Res()
            mu = c.sb([128, 4], F32); r_mu = Res()
            mu2 = c.sb([128, 4], F32); r_mu2 = Res()
            vr = c.sb([128, 4], F32); r_vr = Res()
            nb = c.sb([128, 4], F32); r_nb = Res()
            vn = [c.sb([128, 256], F32) for _ in range(2)]; r_vn = [Res(), Res()]
            vnb = [c.sb([128, 256], BF16) for _ in range(2)]; r_vnb = [Res(), Res()]
            tA = [c.sb([128, 2, 128], F32) for _ in range(2)]; r_tA = [Res(), Res()]
            psM = c.ps([128, 512]); r_psM = Res()
            psQ = c.ps([128, 512]); r_psQ = Res()
            psG = [c.ps([128, 512]) for _ in range(2)]; r_psG = [Res(), Res()]
            pT = pT_s[l].ap()
            pT3 = pT.rearrange("(c p) t -> p c t", p=128)
            hal = agh_out[l].ap().rearrange("(r c p) t -> r p c t", c=4, p=128)
            ib = 0
            for tt in range(NTT):
                t0 = tt * 512
                d = tt % 2
                if tt == 0:
                    prev = lambda cid: (cid + N_CORES - 1) % N_CORES
                    p.dma_dyn("d_azh", lambda cid: (az[0][:, :, 0:H], hal[prev(cid), :, 0:2, :]),
                              reads=[r_agh_out[l]], writes=[r_az[0]])
                    p.dma_dyn("d_gzh", lambda cid: (gz[0][:, :, 0:H], hal[prev(cid), :, 2:4, :]),
                              reads=[r_agh_out[l]], writes=[r_gz[0]])
                    p.dma("sync", "d_az0", az[0][:, :, H:], pT3[:, 0:2, 0:512], reads=[r_pT[l]], writes=[r_az[0]])
                    p.dma("sync", "d_gz0", gz[0][:, :, H:], pT3[:, 2:4, 0:512], reads=[r_pT[l]], writes=[r_gz[0]])
                else:
                    p.dma("sync", f"d_az{d}", az[d][:], pT3[:, 0:2, t0 - H:t0 + 512], reads=[r_pT[l]], writes=[r_az[d]])
                    p.dma("sync", f"d_gz{d}", gz[d][:], pT3[:, 2:4, t0 - H:t0 + 512], reads=[r_pT[l]], writes=[r_gz[d]])
                p.dma("sync", f"d_g4a{d}", g4[d][:, 0:2, :], pT3[:, 6:8, t0:t0 + 512], reads=[r_pT[l]], writes=[r_g4[d]])
                p.dma("sync", f"d_g4b{d}", g4[d][:, 2:4, :], pT3[:, 12:14, t0:t0 + 512], reads=[r_pT[l]], writes=[r_g4[d]])
                p.dma("sync", f"d_uT{d}", uT[d][:], pT3[:, 4:6, t0:t0 + 512], reads=[r_pT[l]], writes=[r_uT[d]])
                p.dma("sync", f"d_vvt{d}", vvt[d][:], vv_s[l].ap()[t0:t0 + 512, :].rearrange("(j p) c -> p j c", p=128),
                      reads=[r_vv[l]], writes=[r_vvt[d]])
                p.op("scalar", lambda e, d=d: e.activation(out=g4[d][:], in_=g4[d][:], func=AF.Silu), reads=[r_g4[d]], writes=[r_g4[d]])
                p.op("scalar", lambda e, d=d: e.activation(out=gz[d][:], in_=gz[d][:], func=AF.Sigmoid), reads=[r_gz[d]], writes=[r_gz[d]])
                p.op("vector", lambda e, d=d: e.tensor_tensor(out=az[d][:], in0=az[d][:], in1=gz[d][:], op=ALU.mult),
                     reads=[r_az[d], r_gz[d]], writes=[r_az[d]])
                if tt == 0:
                    p.op("vector", lambda e: e.tensor_scalar(out=az[0][:, :, 0:H], in0=az[0][:, :, 0:H], scalar1=hmask[:, 0:1],
                                                             scalar2=None, op0=ALU.mult), reads=[r_az[0], r_hmask], writes=[r_az[0]])
                for j in range(31):
                    for ch in range(2):
                        if j == 0:
                            p.op("vector", lambda e, ch=ch, d=d: e.tensor_scalar(
                                out=acc[:, ch, :], in0=az[d][:, ch, 2:514], scalar1=cw[:, ch, 0:1],
                                scalar2=cpar[:, ch:ch + 1], op0=ALU.mult, op1=ALU.add),
                                reads=[r_az[d], r_cw, r_cpar], writes=[r_acc[ch]])
                        else:
                            p.op("vector", lambda e, ch=ch, j=j, d=d: e.scalar_tensor_tensor(
                                out=acc[:, ch, :], in0=az[d][:, ch, 2 + j:514 + j], scalar=cw[:, ch, j:j + 1], in1=acc[:, ch, :],
                                op0=ALU.mult, op1=ALU.add), reads=[r_az[d], r_cw, r_acc[ch]], writes=[r_acc[ch]])
                p.op("vector", lambda e: e.tensor_tensor(out=sq[:], in0=acc[:], in1=acc[:], op=ALU.mult),
                     reads=r_acc, writes=[r_sq])
                for ch in range(2):
                    p.op("tensor", lambda e, ch=ch: e.matmul(psM[:], lhsT=onesm[:], rhs=acc[:, ch, :], start=(ch == 0),
                                                             stop=(ch == 1)), reads=[r_onesm] + r_acc, writes=[r_psM], acc=True)
                for ch in range(2):
                    p.op("tensor", lambda e, ch=ch: e.matmul(psQ[:], lhsT=onesm[:], rhs=sq[:, ch, :], start=(ch == 0),
                                                             stop=(ch == 1)), reads=[r_onesm, r_sq], writes=[r_psQ], acc=True)
                p.op("scalar", lambda e: e.copy(out=mean[:], in_=psM[:]), reads=[r_psM], writes=[r_mean])
                p.op("vector", lambda e: e.tensor_tensor(out=m2[:], in0=mean[:], in1=mean[:], op=ALU.mult),
                     reads=[r_mean], writes=[r_m2])
                p.op("vector", lambda e: e.tensor_tensor(out=var[:], in0=psQ[:], in1=m2[:], op=ALU.subtract),
                     reads=[r_psQ, r_m2], writes=[r_var])
                emit_rsqrt(p, var[:], var[:], 1.0, [r_var], [r_var])
                for ch in range(2):
                    p.op("vector", lambda e, ch=ch: e.tensor_tensor(out=cn[:, ch, :], in0=acc[:, ch, :], in1=mean[:],
                                                                    op=ALU.subtract),
                         reads=[r_acc[ch], r_mean], writes=[r_cn[ch]])
                    p.op("vector", lambda e, ch=ch: e.tensor_tensor(out=cn[:, ch, :], in0=cn[:, ch, :], in1=var[:], op=ALU.mult),
                         reads=[r_cn[ch], r_var], writes=[r_cn[ch]])
                    p.op("scalar", lambda e, ch=ch: e.activation(out=cn[:, ch, :], in_=cn[:, ch, :], func=AF.Silu,
                                                                 scale=cpar[:, 2 + ch:3 + ch], bias=cpar[:, 4 + ch:5 + ch]),
                         reads=[r_cn[ch], r_cpar], writes=[r_cn[ch]])
                    p.op("vector", lambda e, ch=ch, d=d, t0=t0: e.tensor_tensor(out=yK[:, ch, t0:t0 + 512], in0=cn[:, ch, :],
                                                                                in1=g4[d][:, ch, :], op=ALU.mult),
                         reads=[r_cn[ch], r_g4[d]], writes=[r_yK])
                p.op("vector", lambda e: e.memset(ssv[:], 0.0), writes=[r_ssv])
                for j in range(4):
                    p.op("scalar", lambda e, j=j, d=d: e.activation(out=jk[:], in_=vvt[d][:, j, :], func=AF.Square,
                                                                    accum_out=ssv[:, j:j + 1]),
                         reads=[r_vvt[d]], writes=[r_jk, r_ssv])
                p.op("vector", lambda e, d=d: e.reduce_sum(out=s1[:], in_=vvt[d][:], axis=AX.X), reads=[r_vvt[d]], writes=[r_s1])
                p.op("vector", lambda e: e.tensor_scalar(out=mu[:], in0=s1[:], scalar1=1.0 / 256, scalar2=None, op0=ALU.mult),
                     reads=[r_s1], writes=[r_mu])
                p.op("vector", lambda e: e.tensor_tensor(out=mu2[:], in0=mu[:], in1=mu[:], op=ALU.mult),
                     reads=[r_mu], writes=[r_mu2])
                p.op("vector", lambda e: e.scalar_tensor_tensor(out=vr[:], in0=ssv[:], scalar=1.0 / 256, in1=mu2[:],
                                                                op0=ALU.mult, op1=ALU.subtract),
                     reads=[r_ssv, r_mu2], writes=[r_vr])
                emit_rsqrt(p, vr[:], vr[:], 1.0, [r_vr], [r_vr])
                p.op("vector", lambda e: e.scalar_tensor_tensor(out=nb[:], in0=mu[:], scalar=-1.0, in1=vr[:],
                                                                op0=ALU.mult, op1=ALU.mult),
                     reads=[r_mu, r_vr], writes=[r_nb])
                for j in range(4):
                    s = ib % 2
                    ib += 1
                    p.op("scalar", lambda e, j=j, s=s, d=d: e.activation(out=vn[s][:], in_=vvt[d][:, j, :], func=AF.Identity,
                                                                         scale=vr[:, j:j + 1], bias=nb[:, j:j + 1]),
                         reads=[r_vvt[d], r_vr, r_nb], writes=[r_vn[s]])
                    p.op("vector", lambda e, s=s: e.tensor_tensor(out=vn[s][:], in0=vn[s][:], in1=sln[:, 0:256], op=ALU.mult),
                         reads=[r_vn[s], r_sln], writes=[r_vn[s]])
                    p.op("vector", lambda e, s=s: e.tensor_tensor(out=vnb[s][:], in0=vn[s][:], in1=sln[:, 256:512], op=ALU.add),
                         reads=[r_vn[s], r_sln], writes=[r_vnb[s]])
                    for g in range(4):
                        po = (g % 2) * 64
                        p.op("tensor", lambda e, s=s, g=g, po=po: e.matmul(
                            psG[s][po:po + 64, (g // 2) * 128:(g // 2 + 1) * 128], lhsT=vnb[s][:, g * 64:(g + 1) * 64],
                            rhs=wsb[:, g, :], start=True, stop=True), reads=[r_vnb[s], r_wsb], writes=[r_psG[s]], acc=True)
                    p.op("vector", lambda e, s=s: e.tensor_tensor(out=tA[s][:], in0=psG[s][:, 0:256].rearrange("p (m t) -> p m t", m=2),
                                                                  in1=bsb[:], op=ALU.add),
                         reads=[r_psG[s], r_bsb], writes=[r_tA[s]])
                    p.op("vector", lambda e, s=s, j=j, d=d: e.tensor_tensor(out=tA[s][:], in0=tA[s][:],
                                                                            in1=uT[d][:, :, j * 128:(j + 1) * 128], op=ALU.mult),
                         reads=[r_tA[s], r_uT[d]], writes=[r_tA[s]])
                    p.op("vector", lambda e, s=s, j=j, d=d, t0=t0: e.tensor_tensor(
                        out=yK[:, 2:4, t0 + j * 128:t0 + (j + 1) * 128], in0=tA[s][:],
                        in1=g4[d][:, 2:4, j * 128:(j + 1) * 128], op=ALU.mult),
                        reads=[r_tA[s], r_g4[d]], writes=[r_yK])
            p.drain()
            p.emit()

    def phase_C2(l):
        final = l == depth - 1
        xsrc, r_xsrc = (x_d, None) if l == 0 else (x1_s.ap(), r_x1)
        xdst, r_xdst = (out_d, r_out) if final else (x1_s.ap(), r_x1)
        c = Ctx(nc)
        with c.es:
            if final:
                nf, r_nf = load(c, "sync", "nf", [128, D_MODEL], nf_d)
            wob = c.sb([128, 8, 1024], BF16); r_wob = Res()
            wst = [c.sb([128, 1024], F32) for _ in range(2)]; r_wst = [Res(), Res()]
            wo_v = wo_d[l].rearrange("(c p) n -> p c n", p=128)
            for k in range(8):
                s = k % 2
                p.dma("sync", f"d_wst{s}", wst[s][:], wo_v[:, k, :], writes=[r_wst[s]])
                p.op("gpsimd", lambda e, s=s, k=k: e.tensor_copy(out=wob[:, k, :], in_=wst[s][:]),
                     reads=[r_wst[s]], writes=[r_wob])
            g4 = [c.sb([128, 4, 512], F32) for _ in range(2)]; r_g4 = [Res(), Res()]
            ybt = [c.sb([128, 4, 512], F32) for _ in range(2)]; r_ybt = [Res(), Res()]
            xt = [c.sb([128, 4, 1024], F32) for _ in range(2)]; r_xt = [Res(), Res()]
            yA = [c.sb([128, 4, 512], BF16) for _ in range(2)]; r_yA = [Res(), Res()]
            jk = c.sb([128, 1024], F32); r_jk = Res()
            xn = [c.sb([128, 1024], F32) for _ in range(2)]; r_xn = [Res(), Res()]
            ssx = [c.sb([128, 1], F32) for _ in range(2)]; r_ssx = [Res(), Res()]
            xf = [c.sb([128, 1024], F32) for _ in range(2)]; r_xf = [Res(), Res()]
            psX = [c.ps([128, 512]) for _ in range(4)]; r_psX = [Res() for _ in range(4)]
            pT3 = pT_s[l].ap().rearrange("(c p) t -> p c t", p=128)
            yv = [agy_out[l][r].ap().rearrange("(o p) t -> o p t", p=128) for r in range(4)]
            ix = 0
            ipx = 0
            for tt in range(NTT):
                t0 = tt * 512
                d = tt % 2
                p.dma("sync", f"d_g4{d}", g4[d][:], pT3[:, 8:12, t0:t0 + 512], reads=[r_pT[l]], writes=[r_g4[d]])
                for hh in range(4):
                    p.dma_dyn(f"d_ybt{d}", lambda cid, hh=hh, t0=t0, d=d: (ybt[d][:, hh, :],
                                                                          yv[cid % 4][(cid // 4) * 4 + hh, :, t0:t0 + 512]),
                              reads=r_agy_out[l], writes=[r_ybt[d]])
                p.dma("sync", f"d_xt{d}", xt[d][:], xsrc[t0:t0 + 512, :].rearrange("(j p) d -> p j d", p=128),
                      reads=[r_xsrc] if r_xsrc else [], writes=[r_xt[d]])
                p.op("scalar", lambda e, d=d: e.activation(out=g4[d][:], in_=g4[d][:], func=AF.Silu), reads=[r_g4[d]], writes=[r_g4[d]])
                p.op("vector", lambda e, d=d: e.tensor_tensor(out=yA[d][:], in0=ybt[d][:], in1=g4[d][:], op=ALU.mult),
                     reads=[r_ybt[d], r_g4[d]], writes=[r_yA[d]])
                for j in range(4):
                    u = ix % 2
                    ix += 1
                    for n2 in range(2):
                        s = ipx % 4
                        ipx += 1
                        for k in range(8):
                            if k in (0, 1):
                                lh, rl_ = yK[:, k, t0 + j * 128:t0 + (j + 1) * 128], r_yK
                            elif k in (6, 7):
                                lh, rl_ = yK[:, k - 4, t0 + j * 128:t0 + (j + 1) * 128], r_yK
                            else:
                                lh, rl_ = yA[d][:, k - 2, j * 128:(j + 1) * 128], r_yA[d]
                            p.op("tensor", lambda e, s=s, k=k, lh=lh, n2=n2: e.matmul(
                                psX[s][:], lhsT=lh, rhs=wob[:, k, n2 * 512:(n2 + 1) * 512],
                                start=(k == 0), stop=(k == 7)), reads=[rl_, r_wob], writes=[r_psX[s]], acc=True)
                        p.op("vector", lambda e, s=s, u=u, j=j, n2=n2, d=d: e.tensor_tensor(
                            out=xn[u][:, n2 * 512:(n2 + 1) * 512], in0=psX[s][:], in1=xt[d][:, j, n2 * 512:(n2 + 1) * 512], op=ALU.add),
                            reads=[r_psX[s], r_xt[d]], writes=[r_xn[u]])
                    rows = xdst[t0 + j * 128:t0 + (j + 1) * 128, :]
                    if not final:
                        p.dma("sync", f"d_xo{u}", rows, xn[u][:], reads=[r_xn[u]], writes=[r_xdst])
                    else:
                        p.op("vector", lambda e, u=u: e.memset(ssx[u][:], 0.0), writes=[r_ssx[u]])
                        p.op("scalar", lambda e, u=u: e.activation(out=jk[:], in_=xn[u][:], func=AF.Square, accum_out=ssx[u][:]),
                             reads=[r_xn[u]], writes=[r_jk, r_ssx[u]])
                        emit_rsqrt(p, ssx[u][:], ssx[u][:], 1.0 / 1024, [r_ssx[u]], [r_ssx[u]])
                        p.op("vector", lambda e, u=u: e.scalar_tensor_tensor(out=xf[u][:], in0=xn[u][:], scalar=ssx[u][:, 0:1],
                                                                            in1=nf[:], op0=ALU.mult, op1=ALU.mult),
                             reads=[r_xn[u], r_ssx[u], r_nf], writes=[r_xf[u]])
                        p.dma("sync", f"d_xo{u}", rows, xf[u][:], reads=[r_xf[u]], writes=[r_xdst])
            p.drain()
            p.emit()

    pc = Ctx(nc)
    with pc.es:
        yK = pc.sb([128, 4, T], BF16)
        r_yK = Res()
        for l in range(depth):
            phase_P(l)
            phase_C1(l)
            phase_A(l)
            phase_C2(l)
    return nc


_PROGS = {}
bf16 = ml_dtypes.bfloat16


def _run_hw(nc, in_maps):
    res = run_bass_kernel_spmd(nc, in_maps, core_ids=list(range(len(in_maps))))
    return res.results


RUNNER = _run_hw


def kernel(x, norm_g, w_in, conv_w, conv_b, conv_ln_g, conv_ln_b, lam_q1, lam_k1, lam_q2, lam_k2, subln_g,
           sgu_ln_g, sgu_ln_b, w_s, b_s, w_out, norm_f):
    f = lambda a: np.ascontiguousarray(np.asarray(a, dtype=np.float32))
    x = f(x)
    B, S, D = x.shape
    depth = int(np.asarray(w_in).shape[0])
    NTOK = B * S
    T = NTOK // N_CORES
    xf = x.reshape(NTOK, D)
    key = (T, S, depth)
    if key not in _PROGS:
        _PROGS[key] = build_fused(T, S, depth)
    nc = _PROGS[key]
    NR = 4 * (S // 512)
    pidx = np.arange(128, dtype=np.float64)[:, None]
    rel = (np.arange(NR, dtype=np.float64) - (NR - 4))[None, :]
    L = range(depth)
    col = lambda a: f(a).reshape(2, 128).T
    shared = dict(
        w_in=f(w_in),
        g=np.stack([f(norm_g[l]).reshape(8, 128).T for l in L]),
        lamp=np.stack([np.tile(np.concatenate([f(lam_q1[l]), f(lam_k1[l]), f(lam_q2[l]), f(lam_k2[l])])[None], (128, 1))
                       for l in L]),
        gsub=np.stack([np.tile(f(subln_g[l])[None], (128, 1)) for l in L]),
        cst=np.stack([np.tile(np.array([[0.8 - 0.6 * math.exp(-0.3 * l), 1.0 - (0.8 - 0.6 * math.exp(-0.3 * l))]],
                                       np.float32), (128, 1)) for l in L]),
        tri=np.where(np.arange(128)[:, None] > np.arange(128)[None, :], NEG, 0.0).astype(np.float32).astype(bf16),
        idn=np.eye(128, dtype=np.float32).astype(bf16),
        idf=np.eye(128, dtype=np.float32),
        wo=f(w_out),
        cw=np.stack([f(conv_w[l]).T.reshape(2, 128, 31).transpose(1, 0, 2) for l in L]),
        cpar=np.stack([np.concatenate([col(conv_b[l]), col(conv_ln_g[l]), col(conv_ln_b[l])], 1) for l in L]),
        sln=np.stack([np.concatenate([np.tile(f(sgu_ln_g[l])[None], (128, 1)), np.tile(f(sgu_ln_b[l])[None], (128, 1))], 1)
                      for l in L]),
        wsT=np.stack([f(w_s[l]).transpose(2, 0, 1) for l in L]),
        trilm=np.tril(np.ones((128, 128), np.float32)).T.copy(),
        bsb=np.stack([np.repeat(f(b_s[l]).reshape(2, 2, 128), 64, axis=1).transpose(1, 0, 2) for l in L]),
        nf=np.tile(f(norm_f)[None], (128, 1)),
        onesm=np.full((128, 128), 1.0 / 256, np.float32),
    )
    shared = {k: np.ascontiguousarray(v) for k, v in shared.items()}
    ins = []
    for c in range(N_CORES):
        h = c % 4
        slope = 2.0 ** (-8.0 * (h + 1) / 4)
        d = dict(shared)
        d["x"] = xf[c * T:(c + 1) * T]
        d["abias"] = (slope * (128.0 * rel + pidx - 256.0)).astype(np.float32)
        d["hmask"] = np.full((128, 1), 0.0 if (c * T) % S == 0 else 1.0, np.float32)
        ins.append(d)
    outs = RUNNER(nc, ins)
    return np.concatenate([np.asarray(o["out"]) for o in outs], 0).reshape(B, S, D).astype(np.float32)
```
